# Optimizing a Trainium2 kernel written in Bass

```python
import jax, jax.numpy as jnp
from jax import lax
import numpy as np

D_MODEL = 1024
BATCH = 8
SEQ = 4096
DEPTH = 4

N_MIXERS = 4
GROUP_WIDTH = D_MODEL // N_MIXERS
HEAD_DIM = 64
FOX_HEADS = GROUP_WIDTH // HEAD_DIM
RET_HEADS = GROUP_WIDTH // HEAD_DIM
POOL_WINDOWS = (2, 4, 8, 16)
POOL_GROUPS = len(POOL_WINDOWS)
POOL_GROUP_DIM = GROUP_WIDTH // POOL_GROUPS
CONV_WIDTH = 31
CONV_CH = GROUP_WIDTH
N_IN = 3 * GROUP_WIDTH + FOX_HEADS + GROUP_WIDTH + 4 * GROUP_WIDTH + 2 * CONV_CH
D_FF = ((8 * D_MODEL + 3 * 256 - 1) // (3 * 256)) * 256
Q_BLOCK = 128
RET_CHUNK = 128
ROPE_BASE = 10000.0
EPS = 1e-6

kernel_name = "hymba_style_fox_pool_retnet_conformer_trunk"


def rmsnorm(x, g):
    xf = x.astype(jnp.float32)
    y = xf * lax.rsqrt(jnp.mean(xf * xf, axis=-1, keepdims=True) + EPS)
    return (y * g.astype(jnp.float32)).astype(x.dtype)


def layernorm(x, g, b):
    xf = x.astype(jnp.float32)
    mu = jnp.mean(xf, axis=-1, keepdims=True)
    var = jnp.mean(jnp.square(xf - mu), axis=-1, keepdims=True)
    y = (xf - mu) * lax.rsqrt(var + EPS)
    return (y * g.astype(jnp.float32) + b.astype(jnp.float32)).astype(x.dtype)


def rope(x, positions):
    half = x.shape[-1] // 2
    inv = ROPE_BASE ** (-jnp.arange(half, dtype=jnp.float32) / half)
    ang = positions.astype(jnp.float32)[..., None] * inv
    cos = jnp.cos(ang)[:, :, None, :]
    sin = jnp.sin(ang)[:, :, None, :]
    x1, x2 = x[..., :half], x[..., half:]
    return jnp.concatenate([x1 * cos - x2 * sin, x1 * sin + x2 * cos], axis=-1).astype(x.dtype)


def fox_attention(q, k, v, f_logit, f_bias):
    B, S, H, Dh = q.shape
    log_f = jax.nn.log_sigmoid((f_logit + f_bias).astype(jnp.float32))
    cum = jnp.cumsum(log_f, axis=1).transpose(0, 2, 1)
    key_pos = jnp.arange(S)
    scale = Dh ** -0.5

    def block(i):
        start = i * Q_BLOCK
        qb = lax.dynamic_slice_in_dim(q, start, Q_BLOCK, axis=1)
        cq = lax.dynamic_slice_in_dim(cum, start, Q_BLOCK, axis=2)
        s = jnp.einsum('bqhd,bkhd->bhqk', qb, k).astype(jnp.float32) * scale
        s = s + cq[..., :, None] - cum[:, :, None, :]
        q_pos = start + jnp.arange(Q_BLOCK)
        causal = key_pos[None, :] <= q_pos[:, None]
        s = jnp.where(causal, s, -jnp.inf)
        p = jax.nn.softmax(s, axis=-1).astype(v.dtype)
        return jnp.einsum('bhqk,bkhd->bqhd', p, v)

    out = lax.map(block, jnp.arange(S // Q_BLOCK))
    return out.transpose(1, 0, 2, 3, 4).reshape(B, S, H * Dh)


def pool_mixer(u, pool_w, pool_scale):
    B, S, _ = u.shape
    ug = u.reshape(B, S, POOL_GROUPS, POOL_GROUP_DIM)
    cs = jnp.cumsum(ug.astype(jnp.float32), axis=1)
    pos = jnp.arange(S)
    means = []
    for gi, w in enumerate(POOL_WINDOWS):
        c_g = cs[:, :, gi]
        shifted = jnp.pad(c_g, ((0, 0), (w, 0), (0, 0)))[:, :S]
        count = jnp.minimum(pos + 1, w).astype(jnp.float32)[None, :, None]
        means.append((c_g - shifted) / count)
    pooled = jnp.stack(means, axis=2)
    delta = (pooled - ug.astype(jnp.float32)).astype(u.dtype)
    mixed = jnp.einsum('bsgc,gcd->bsgd', delta, pool_w).reshape(B, S, GROUP_WIDTH)
    return mixed * pool_scale


def retention(q, k, v, g, positions, gn_g):
    B, S, H, Dh = q.shape
    q = rope(q, positions)
    k = rope(k, positions) * (Dh ** -0.5)
    log_gamma = jnp.log(1.0 - 2.0 ** (-5.0 - jnp.arange(H, dtype=jnp.float32)))
    C = RET_CHUNK
    NC = S // C
    idx = jnp.arange(C, dtype=jnp.float32)
    diff = idx[:, None] - idx[None, :]
    dmask = jnp.where(diff >= 0, jnp.exp(log_gamma[:, None, None] * jnp.maximum(diff, 0.0)), 0.0)
    zeta = jnp.exp(log_gamma[:, None] * (C - 1 - idx))
    xi = jnp.exp(log_gamma[:, None] * (idx + 1))
    chunk_decay = jnp.exp(log_gamma * C)

    qc = q.reshape(B, NC, C, H, Dh)
    kc = k.reshape(B, NC, C, H, Dh)
    vc = v.reshape(B, NC, C, H, Dh)
    intra_s = jnp.einsum('bnqhd,bnkhd->bnhqk', qc, kc) * dmask
    intra = jnp.einsum('bnhqk,bnkhe->bnqhe', intra_s, vc)
    kv = jnp.einsum('bnkhd,hk,bnkhe->bnhde', kc, zeta, vc)

    def step(state, xs):
        q_n, kv_n = xs
        cross = jnp.einsum('bqhd,bhde->bqhe', q_n, state)
        new = (state * chunk_decay[None, :, None, None] + kv_n).astype(state.dtype)
        return new, cross

    state0 = jnp.zeros((B, H, Dh, Dh), dtype=kv.dtype)
    _, cross = lax.scan(step, state0, (qc.transpose(1, 0, 2, 3, 4), kv.transpose(1, 0, 2, 3, 4)))
    cross = cross.transpose(1, 0, 2, 3, 4) * xi.T[None, None, :, :, None]
    o = (intra + cross).reshape(B, S, H, Dh).astype(jnp.float32)
    mu = jnp.mean(o, axis=-1, keepdims=True)
    var = jnp.mean(jnp.square(o - mu), axis=-1, keepdims=True)
    o = (o - mu) * lax.rsqrt(var + EPS) * gn_g.reshape(H, Dh).astype(jnp.float32)
    o = o.reshape(B, S, H * Dh).astype(g.dtype)
    return jax.nn.silu(g) * o


def conformer_conv(u, conv_w, conv_b, ln_g, ln_b):
    a, gate = jnp.split(u, 2, axis=-1)
    h = a * jax.nn.sigmoid(gate)
    h = lax.conv_general_dilated(
        h, conv_w[:, None, :], window_strides=(1,), padding=[(CONV_WIDTH - 1, 0)],
        dimension_numbers=('NWC', 'WIO', 'NWC'), feature_group_count=CONV_CH) + conv_b
    return jax.nn.silu(layernorm(h, ln_g, ln_b))


def setup_inputs(seed: int = 0) -> dict:
    key = jax.random.key(seed)
    ks = jax.random.split(key, 24)
    f32 = jnp.float32
    D, G = D_MODEL, GROUP_WIDTH
    nrm = lambda k, shape, s: jax.random.normal(k, shape, f32) * s
    return {
        "x": jax.random.normal(ks[0], (BATCH, SEQ, D), f32),
        "c": jax.random.normal(ks[1], (BATCH, D), f32),
        "positions": jnp.broadcast_to(jnp.arange(SEQ, dtype=jnp.int32)[None, :], (BATCH, SEQ)),
        "ada_w": nrm(ks[2], (DEPTH, D, 6 * D), 0.5 * D ** -0.5),
        "ada_b": nrm(ks[3], (DEPTH, 6 * D), 0.01),
        "norm_mix_g": 1.0 + nrm(ks[4], (DEPTH, D), 0.1),
        "norm_ffn_g": 1.0 + nrm(ks[5], (DEPTH, D), 0.1),
        "w_in": nrm(ks[6], (DEPTH, D, N_IN), D ** -0.5),
        "fox_fb": jax.random.uniform(ks[7], (DEPTH, FOX_HEADS), f32, 2.0, 5.0),
        "pool_w": nrm(ks[8], (DEPTH, POOL_GROUPS, POOL_GROUP_DIM, POOL_GROUP_DIM), POOL_GROUP_DIM ** -0.5),
        "pool_scale": 1.0 + nrm(ks[9], (DEPTH, G), 0.1),
        "ret_gn_g": 1.0 + nrm(ks[10], (DEPTH, G), 0.1),
        "conv_w": nrm(ks[11], (DEPTH, CONV_WIDTH, CONV_CH), CONV_WIDTH ** -0.5),
        "conv_b": nrm(ks[12], (DEPTH, CONV_CH), 0.01),
        "conv_ln_g": 1.0 + nrm(ks[13], (DEPTH, CONV_CH), 0.1),
        "conv_ln_b": nrm(ks[14], (DEPTH, CONV_CH), 0.01),
        "w_out": nrm(ks[15], (DEPTH, D, D), D ** -0.5),
        "ffn_w1": nrm(ks[16], (DEPTH, D, D_FF), D ** -0.5),
        "ffn_w3": nrm(ks[17], (DEPTH, D, D_FF), D ** -0.5),
        "ffn_w2": nrm(ks[18], (DEPTH, D_FF, D), D_FF ** -0.5),
        "final_g": 1.0 + nrm(ks[19], (D,), 0.1),
    }


def reference(x, c, positions, ada_w, ada_b, norm_mix_g, norm_ffn_g, w_in, fox_fb,
              pool_w, pool_scale, ret_gn_g, conv_w, conv_b, conv_ln_g, conv_ln_b,
              w_out, ffn_w1, ffn_w3, ffn_w2, final_g):
    B, S, _ = x.shape
    G, H, Dh = GROUP_WIDTH, FOX_HEADS, HEAD_DIM
    bounds = [G, 2 * G, 3 * G, 3 * G + H,
              4 * G + H,
              5 * G + H, 6 * G + H, 7 * G + H, 8 * G + H]
    c_act = jax.nn.silu(c)
    for l in range(DEPTH):
        mod = (c_act @ ada_w[l] + ada_b[l])[:, None, :]
        sh1, sc1, g1, sh2, sc2, g2 = jnp.split(mod, 6, axis=-1)

        h = rmsnorm(x, norm_mix_g[l]) * (1.0 + sc1) + sh1
        u = h @ w_in[l]
        fq, fk, fv, ff, pu, rq, rk, rv, rg, cu = jnp.split(u, bounds, axis=-1)
        y_fox = fox_attention(fq.reshape(B, S, H, Dh), fk.reshape(B, S, H, Dh),
                              fv.reshape(B, S, H, Dh), ff, fox_fb[l])
        y_pool = pool_mixer(pu, pool_w[l], pool_scale[l])
        y_ret = retention(rq.reshape(B, S, RET_HEADS, Dh), rk.reshape(B, S, RET_HEADS, Dh),
                          rv.reshape(B, S, RET_HEADS, Dh), rg, positions, ret_gn_g[l])
        y_conv = conformer_conv(cu, conv_w[l], conv_b[l], conv_ln_g[l], conv_ln_b[l])
        mix = jnp.concatenate([y_fox, y_pool, y_ret, y_conv], axis=-1)
        x = x + g1 * (mix @ w_out[l])

        h = rmsnorm(x, norm_ffn_g[l]) * (1.0 + sc2) + sh2
        f = (jax.nn.silu(h @ ffn_w1[l]) * (h @ ffn_w3[l])) @ ffn_w2[l]
        x = x + g2 * f
    return rmsnorm(x, final_g)
```

```python
import numpy as np
from contextlib import ExitStack
import concourse.bass as bass
import concourse.mybir as mybir
from concourse.bass_utils import run_bass_kernel_spmd

F32 = mybir.dt.float32
BF16 = mybir.dt.bfloat16
I32 = mybir.dt.int32
AF = mybir.ActivationFunctionType
ALU = mybir.AluOpType

D = 1024
KC = 8
DFF = 2816
MC = 22
NIN = 2564
EPS = 1e-6
TB = 512
TA = 256
FQ, FK, FV, FFO, PU, RQ, RK, RV, RG, CA, CG = 0, 256, 512, 768, 772, 1028, 1284, 1540, 1796, 2052, 2308

ENGS = ("pe", "act", "dve", "pool", "sp")


class _Op:
    __slots__ = ("eng", "dma_key", "idx", "eidx", "deps", "signal", "sigkey", "sigval", "epoch")


class Trk:
    def __init__(self, plan=None, nc=None, sem_of=None):
        self.plan = plan
        self.nc = nc
        self.sem_of = sem_of
        self.n = 0
        if plan is None:
            self.ops = []
            self.last_w = {}
            self.readers = {}
            self.ecount = {e: 0 for e in ENGS}
            self.elast = {e: None for e in ENGS}
            self.dma_last = {}
            self.rot = {}
            self.epoch = 0
        else:
            self.engs = {"pe": nc.tensor, "act": nc.scalar, "dve": nc.vector,
                         "pool": nc.gpsimd, "sp": nc.sync}

    def op(self, eng, fn, reads=(), writes=(), dma_key=None):
        if self.plan is not None:
            waits, sig = self.plan[self.n]
            self.n += 1
            e = self.engs[eng]
            for k, v in waits:
                e.wait_ge(self.sem_of(k), v)
            ins = fn(e)
            if sig is not None:
                ins.then_inc(self.sem_of(sig[0]), sig[1])
            return None
        extra = None
        if type(dma_key) is tuple:
            base, k = dma_key
            i = self.rot.get(base, 0)
            self.rot[base] = i + 1
            dma_key = base + "_r" + str(i % k)
            extra = self.dma_last.get(dma_key)
        o = _Op()
        o.eng = eng
        o.dma_key = dma_key
        o.idx = len(self.ops)
        o.epoch = self.epoch
        o.eidx = self.ecount[eng]
        if dma_key is None:
            self.ecount[eng] += 1
        o.signal = dma_key is not None
        pr_ = [r for r in reads if r == "psM" or (type(r) is tuple and r[0] in ("psf", "psb"))]
        if pr_:
            writes = tuple(writes) + tuple(r for r in pr_ if r not in writes)
        deps = set()
        for r in reads:
            w = self.last_w.get(r)
            if w is not None:
                deps.add(w)
        for r in writes:
            w = self.last_w.get(r)
            if w is not None:
                deps.add(w)
            rs = self.readers.get(r)
            if rs:
                deps.update(rs)
        for r in reads:
            self.readers.setdefault(r, []).append(o)
        for r in writes:
            self.last_w[r] = o
            self.readers[r] = []
        if extra is not None:
            deps.add(extra)
        o.deps = deps
        self.ops.append(o)
        if dma_key is None:
            self.elast[eng] = o
        else:
            self.dma_last[dma_key] = o
        return o

    def barrier(self):
        if self.plan is not None:
            for e in ENGS:
                self.op(e, lambda en: en.nop())
            return
        deps = set(o for o in self.elast.values() if o is not None)
        deps.update(self.dma_last.values())
        for e in ENGS:
            o = self.op(e, None)
            o.deps = set(deps)
        self.last_w = {}
        self.readers = {}
        self.epoch += 1

    def finish(self):
        ops = self.ops
        for o in ops:
            real = []
            for d in o.deps:
                if d is o:
                    continue
                if d.dma_key is None and o.dma_key is None and d.eng == o.eng:
                    if o.eng == "pe" or o.eidx - d.eidx >= 2:
                        continue
                real.append(d)
            best = {}
            for d in real:
                k = ("D", d.dma_key) if d.dma_key is not None else ("E", d.eng)
                b = best.get(k)
                if b is None or d.idx > b.idx:
                    best[k] = d
            real = list(best.values())
            o.deps = real
            for d in real:
                d.signal = True
        cnt = {}
        for o in ops:
            if o.signal:
                if o.dma_key is not None:
                    k = "D:" + o.dma_key
                    inc = 16
                else:
                    k = "E:" + o.eng
                    inc = 1
                cnt[k] = cnt.get(k, 0) + inc
                o.sigkey = k
                o.sigval = cnt[k]
        seen = {e: {} for e in ENGS}
        plan = []
        for o in ops:
            w = {}
            for d in o.deps:
                if w.get(d.sigkey, 0) < d.sigval:
                    w[d.sigkey] = d.sigval
            waits = []
            sd = seen[o.eng]
            for k, v in w.items():
                if sd.get(k, 0) >= v:
                    continue
                sd[k] = v
                waits.append((k, v))
            sig = None
            if o.signal:
                sig = (o.sigkey, 16 if o.dma_key is not None else 1)
            plan.append((waits, sig))
        self.semkeys = sorted(cnt.keys())
        self.counts = cnt
        return plan


class Ring:
    def __init__(self, items):
        self.items = items
        self.i = 0

    def next(self):
        it = self.items[self.i % len(self.items)]
        self.i += 1
        return it


def _const_layout():
    names = [("ident", 128), ("rm", 128), ("bd64", 128), ("o256", 128), ("sh", 128),
             ("dmask", 512), ("xi", 256), ("zeta", 256), ("decay", 2), ("poolrw", 2),
             ("rc16", 32), ("invf", 1), ("maskT", 128), ("ones", 128)]
    off = {}
    o = 0
    for n, w in names:
        off[n] = (o, w)
        o += w
    return off, o


CO, NCONST = _const_layout()


def make_consts():
    c = np.zeros((128, NCONST), np.float32)

    def put(name, arr):
        o, w = CO[name]
        c[:arr.shape[0], o:o + w] = arr

    put("ident", np.eye(128, dtype=np.float32))
    rm = np.zeros((128, 128), np.float32)
    for m in range(128):
        if (m % 64) < 32:
            rm[m + 32, m] = -1.0
        else:
            rm[m - 32, m] = 1.0
    put("rm", rm)
    bd = np.zeros((128, 128), np.float32)
    bd[:64, :64] = 1.0 / 64
    bd[64:, 64:] = 1.0 / 64
    put("bd64", bd)
    put("o256", np.full((128, 128), 1.0 / 256, np.float32))
    sh = np.zeros((128, 128), np.float32)
    for k in range(128):
        sh[k, (k + 64) % 128] = 1.0
    put("sh", sh)
    lg = np.log(1.0 - 2.0 ** (-5.0 - np.arange(4, dtype=np.float32))).astype(np.float32)
    idx = np.arange(128, dtype=np.float32)
    dm = np.zeros((128, 512), np.float32)
    for h in range(4):
        diff = idx[None, :] - idx[:, None]
        dm[:, h * 128:(h + 1) * 128] = np.where(diff >= 0, np.exp(lg[h] * np.maximum(diff, 0.0)), 0.0) * 0.125
    put("dmask", dm)
    xi = np.zeros((128, 256), np.float32)
    ze = np.zeros((128, 256), np.float32)
    dec = np.zeros((128, 2), np.float32)
    for pr in range(2):
        for half in range(2):
            h = 2 * pr + half
            xi[half * 64:(half + 1) * 64, pr * 128:(pr + 1) * 128] = np.exp(lg[h] * (idx + 1))[None, :]
            ze[:, pr * 128 + half * 64: pr * 128 + (half + 1) * 64] = (np.exp(lg[h] * (127 - idx)) * 0.125)[:, None]
            dec[half * 64:(half + 1) * 64, pr] = np.exp(lg[h] * 128)
    put("xi", xi)
    put("zeta", ze)
    put("decay", dec)
    prw = np.zeros((128, 2), np.float32)
    rc = np.zeros((128, 32), np.float32)
    wins = (2, 4, 8, 16)
    for ch in range(2):
        for half in range(2):
            w = wins[2 * ch + half]
            prw[half * 64:(half + 1) * 64, ch] = 1.0 / w
            rc[half * 64:(half + 1) * 64, ch * 16:(ch + 1) * 16] = (1.0 / np.minimum(np.arange(16) + 1, w))[None, :]
    put("poolrw", prw)
    put("rc16", rc)
    inv = (10000.0 ** (-np.arange(32, dtype=np.float32) / 32)).astype(np.float32)
    put("invf", inv[np.arange(128) % 32][:, None])
    mk = np.where(idx[:, None] > idx[None, :], -30000.0, 0.0).astype(np.float32)
    put("maskT", mk)
    put("ones", np.ones((128, 128), np.float32))
    return c


PPO = {}


def _pp_layout():
    names = [("gmix", 8), ("gffn", 8), ("adab", 48), ("fb", 1), ("pscale", 2), ("gng", 2),
             ("cb", 2), ("lng", 2), ("lnb", 2), ("cw", 62), ("poolbd", 256)]
    o = 0
    for n, w in names:
        PPO[n] = (o, w)
        o += w
    return o


NPP = _pp_layout()


def make_pp(inp, depth):
    pp = np.zeros((depth, 128, NPP), np.float32)

    def col(v, nch):
        return np.ascontiguousarray(np.asarray(v, np.float32).reshape(nch, 128).T)

    for l in range(depth):
        def put(name, arr):
            o, w = PPO[name]
            pp[l, :arr.shape[0], o:o + w] = arr
        put("gmix", col(inp["norm_mix_g"][l], 8))
        put("gffn", col(inp["norm_ffn_g"][l], 8))
        put("adab", col(inp["ada_b"][l], 48))
        put("fb", np.asarray(inp["fox_fb"][l], np.float32).reshape(4, 1))
        put("pscale", col(inp["pool_scale"][l], 2))
        put("gng", col(inp["ret_gn_g"][l], 2))
        put("cb", col(inp["conv_b"][l], 2))
        put("lng", col(inp["conv_ln_g"][l], 2))
        put("lnb", col(inp["conv_ln_b"][l], 2))
        cw = np.asarray(inp["conv_w"][l], np.float32)
        cwt = cw.T.reshape(2, 128, 31).transpose(1, 0, 2).reshape(128, 62)
        put("cw", cwt)
        pw = np.asarray(inp["pool_w"][l], np.float32)
        bdm = np.zeros((128, 256), np.float32)
        for ch in range(2):
            for half in range(2):
                g = 2 * ch + half
                bdm[half * 64:(half + 1) * 64, ch * 128 + half * 64: ch * 128 + (half + 1) * 64] = pw[g]
        put("poolbd", bdm)
    return pp


def build_program(nc, T, S, DEPTH, phases=("A", "B"), mixers=("fox", "pool", "ret", "conv"),
                  final_norm=True, dbg=None):
    NTB = S // TB
    NTA = S // TA
    dt = nc.dram_tensor
    x_in = dt("x", [S, D], F32, kind="ExternalInput").ap()
    ccol = dt("ccol", [128, 8], F32, kind="ExternalInput").ap()
    pos_in = dt("pos", [1, S], I32, kind="ExternalInput").ap()
    ada_w = dt("ada_w", [DEPTH, D, 6 * D], F32, kind="ExternalInput").ap()
    w_in = dt("w_in", [DEPTH, D, NIN], F32, kind="ExternalInput").ap()
    w_out = dt("w_out", [DEPTH, D, D], F32, kind="ExternalInput").ap()
    w1 = dt("ffn_w1", [DEPTH, D, DFF], F32, kind="ExternalInput").ap()
    w3 = dt("ffn_w3", [DEPTH, D, DFF], F32, kind="ExternalInput").ap()
    w2 = dt("ffn_w2", [DEPTH, DFF, D], F32, kind="ExternalInput").ap()
    pp_in = dt("pp", [DEPTH, 128, NPP], F32, kind="ExternalInput").ap()
    cst_in = dt("cst", [128, NCONST], F32, kind="ExternalInput").ap()
    fg_in = dt("fg", [1, D], F32, kind="ExternalInput").ap()
    out = dt("out", [S, D], F32, kind="ExternalOutput").ap()
    xs = dt("xs", [S, D], F32, kind="Internal").ap()
    kc_d = dt("kc_d", [65, 4, S], BF16, kind="Internal").ap()
    vc_d = dt("vc_d", [S, 4, 128], BF16, kind="Internal").ap()
    cs_d = dt("cs_d", [2, 128, S], F32, kind="Internal").ap()

    top = ExitStack()
    with top:
        uid = [0]

        def sb(name, shape, dtype, stack=top):
            uid[0] += 1
            return stack.enter_context(nc.sbuf_tensor(f"s{uid[0]}_{name}", shape, dtype))

        def pst(name, shape, dtype, stack=top):
            return stack.enter_context(nc.psum_tensor("p_" + name, shape, dtype))

        cst = sb("cst", [128, NCONST], F32)
        identb = sb("identb", [128, 128], BF16)
        maskb = sb("maskb", [128, 128], BF16)
        pp = [sb(f"pp{i}", [128, NPP], F32) for i in range(2)]
        cact = sb("cact", [128, 8], F32)
        modc = [sb(f"modc{i}", [128, 48], F32) for i in range(2)]
        scol = [sb(f"scol{i}", [128, 16], F32) for i in range(2)]
        fgrow = sb("fgrow", [1, D], F32)
        rstd_t = sb("rstd_t", [128, 16], F32)
        ss_t = sb("ss_t", [128, 16], F32)

        def C(name, rows=128):
            o, w = CO[name]
            return cst[0:rows, o:o + w]

        def PP(l, name, rows=128):
            o, w = PPO[name]
            return pp[l % 2][0:rows, o:o + w]

        NPF = 5
        psF = [pst(f"psf{i}", [128, 512], F32) for i in range(NPF)]
        psM = pst("psm", [128, 512], F32)
        psB = [pst(f"psb{i}", [128, 1024], BF16) for i in range(2)]
        psring = Ring([(psF[i], ("psf", i)) for i in range(NPF)])
        psbring = Ring([(psB[i][:, 0:512], ("psb", i)) for i in range(2)])

        def dma(q, o, i, reads, writes, key, **kw):
            T.op(q, lambda e: e.dma_start(out=o, in_=i, **kw), reads, writes, dma_key=key)

        def mm(o, l, r, st, sp_, reads, writes, **kw):
            T.op("pe", lambda e: e.matmul(o, l, r, start=st, stop=sp_, **kw), reads, writes)

        def tr(o, i, idn, reads, writes):
            T.op("pe", lambda e: e.transpose(o, i, idn), reads, writes)

        def act(o, i, f, reads, writes, **kw):
            T.op("act", lambda e: e.activation(out=o, in_=i, func=f, **kw), reads, writes)

        def tt(eng, o, a, b, op, reads, writes):
            T.op(eng, lambda e: e.tensor_tensor(out=o, in0=a, in1=b, op=op), reads, writes)

        def ts(eng, o, a, s1, s2, op0, op1, reads, writes):
            if op1 is None:
                T.op(eng, lambda e: e.tensor_scalar(out=o, in0=a, scalar1=s1, scalar2=None, op0=op0),
                     reads, writes)
            else:
                T.op(eng, lambda e: e.tensor_scalar(out=o, in0=a, scalar1=s1, scalar2=s2, op0=op0, op1=op1),
                     reads, writes)

        def stt(o, a, s, b, op0, op1, reads, writes):
            T.op("dve", lambda e: e.scalar_tensor_tensor(out=o, in0=a, scalar=s, in1=b, op0=op0, op1=op1),
                 reads, writes)

        def cp(eng, o, i, reads, writes):
            if eng == "act":
                T.op(eng, lambda e: e.activation(out=o, in_=i, func=AF.Copy), reads, writes)
            else:
                T.op(eng, lambda e: e.tensor_copy(out=o, in_=i), reads, writes)

        def recip(o, i, reads, writes):
            T.op("dve", lambda e: e.reciprocal(out=o, in_=i), reads, writes)

        def mset(eng, o, v, writes):
            T.op(eng, lambda e: e.memset(o, v), (), writes)

        dma("sp", cst[:], cst_in, (), ["cst"], "cst")
        dma("sp", cact[:], ccol, (), ["cact"], "cact")
        dma("sp", fgrow[:], fg_in, (), ["fgrow"], "fgrow")
        cp("dve", identb[:], C("ident"), ["cst"], ["identb"])
        cp("dve", maskb[:], C("maskT"), ["cst"], ["maskb"])
        act(cact[:], cact[:], AF.Silu, ["cact"], ["cact"])

        def load_pp(l):
            dma("sp", pp[l % 2][:], pp_in[l], (), [("pp", l % 2)], f"pp{l % 2}")

        def mod_chunk(l, kc, stg):
            for piece in range(6):
                buf, bkey = stg.next()
                dma("sp", buf[:, 0:1024], ada_w[l, kc * 128:(kc + 1) * 128, piece * 1024:(piece + 1) * 1024],
                    (), [bkey], bkey[0] + str(bkey[1]))
                for j in range(8):
                    colx = piece * 8 + j
                    first = (kc == 0 and piece == 0 and j == 0)
                    mm(psM[:, colx:colx + 1], buf[:, j * 128:(j + 1) * 128], cact[:, kc:kc + 1],
                       first, kc == 7, [bkey, "cact"], ["psM"], skip_group_check=True)

        def mod_finish(l):
            m = modc[l % 2]
            sc = scol[l % 2]
            tt("dve", m[:], psM[:, 0:48], PP(l, "adab"), ALU.add, ["psM", ("pp", l % 2)], [("modc", l % 2)])
            stt(sc[:, 0:8], m[:, 8:16], 1.0, PP(l, "gmix"), ALU.add, ALU.mult,
                [("modc", l % 2), ("pp", l % 2)], [("scol", l % 2, 0)])
            stt(sc[:, 8:16], m[:, 32:40], 1.0, PP(l, "gffn"), ALU.add, ALU.mult,
                [("modc", l % 2), ("pp", l % 2)], [("scol", l % 2, 1)])

        def gate_rows(l, which, gdst, gkey, scratch, skey):
            base = 16 if which == 1 else 40
            m = modc[l % 2]
            for half in range(2):
                pb_, pk = psring.next()
                for jj in range(4):
                    j = half * 4 + jj
                    cp("dve", scratch[:, jj * 128:(jj + 1) * 128],
                       m[:, base + j:base + j + 1].to_broadcast([128, 128]),
                       [("modc", l % 2)], [skey])
                    mm(pb_[:, jj * 128:(jj + 1) * 128], scratch[:, jj * 128:(jj + 1) * 128], C("ident"),
                       True, True, [skey, "cst"], [pk])
                cp("dve", gdst[:, half * 512:(half + 1) * 512], pb_[:], [pk], [gkey])

        def front_end(src, r0, nsub, xin_ring, xn_ring, hT, hkey, sc_ap, sh_ap, sckeys):
            xns = []
            for j in range(nsub):
                buf, bkey = xin_ring.next()
                dma("sp", buf[:], src[r0 + j * 128: r0 + (j + 1) * 128, :], [("xrows", (r0 // 128) + j)],
                    [bkey], bkey[0] + str(bkey[1]))
                xn, xk = xn_ring.next()
                T.op("act", (lambda b, jj, xo: (lambda e: e.activation(
                    out=xo[:], in_=b[:], func=AF.Square, accum_out=ss_t[:, jj:jj + 1])))(buf, j, xn),
                    [bkey], [xk, ("ss", j)])
                if dbg == "F1":
                    continue
                ts("dve", rstd_t[:, j:j + 1], ss_t[:, j:j + 1], 1.0 / D, EPS, ALU.mult, ALU.add,
                   [("ss", j)], [("rstd", j)])
                act(rstd_t[:, j:j + 1], rstd_t[:, j:j + 1], AF.Sqrt, [("rstd", j)], [("rstd", j)])
                recip(rstd_t[:, j:j + 1], rstd_t[:, j:j + 1], [("rstd", j)], [("rstd", j)])
                act(xn[:], buf[:], AF.Copy, [bkey, ("rstd", j)], [xk], scale=rstd_t[:, j:j + 1])
                xns.append((xn, xk))
            if dbg in ("F1", "F2"):
                return
            for c in range(KC):
                pb_, pk = psbring.next()
                for j, (xn, xk) in enumerate(xns):
                    tr(pb_[:, j * 128:(j + 1) * 128], xn[:, c * 128:(c + 1) * 128], identb[:],
                       [xk, "identb"], [pk])
                act(hT[:, c, 0:nsub * 128], pb_[:, 0:nsub * 128], AF.Identity, [pk] + sckeys, [(hkey, c)],
                    scale=sc_ap[:, c:c + 1], bias=sh_ap[:, c:c + 1])

        def phase_B(l, src, dst_final):
            last = dst_final
            with ExitStack() as ph:
                W1 = sb("W1", [128, KC, DFF], BF16, ph)
                W3 = sb("W3", [128, KC, DFF], BF16, ph)
                W2g = sb("W2g", [128, MC, D], BF16, ph)
                xin_t = [sb(f"xinB{i}", [128, 1024], F32, ph) for i in range(3)]
                xin = Ring([(xin_t[i], ("xin", i)) for i in range(3)])
                stg = xin
                xn_t = [sb(f"xnB{i}", [128, 1024], BF16, ph) for i in range(4)]
                xnr = Ring([(xn_t[i], ("xn", i)) for i in range(4)])
                hT = sb("hTB", [128, KC, TB], BF16, ph)
                gT = sb("gTB", [128, MC, TB], BF16, ph)
                sa_t = [sb(f"saB{i}", [128, 512], F32, ph) for i in range(2)]
                sar = Ring([(sa_t[i], ("sa", i)) for i in range(2)])
                G2 = sb("G2", [128, D], F32, ph)
                m = modc[l % 2]
                sc = scol[l % 2]
                gate_rows(l, 2, G2, "G2", xin_t[0], ("xin", 0))
                w1v = w1[l].rearrange("(c p) n -> p c n", p=128)
                w3v = w3[l].rearrange("(c p) n -> p c n", p=128)
                for blk in range(11):
                    csl = slice(blk * 256, (blk + 1) * 256)
                    dma("pool", W1[:, :, csl], w1v[:, :, csl], (), [("W1", blk)], ("W1", 4))
                    dma("pool", W3[:, :, csl], w3v[:, :, csl], (), [("W3", blk)], ("W3", 4))
                for mch in range(MC):
                    buf, bkey = stg.next()
                    dma("sp", buf[:], w2[l, mch * 128:(mch + 1) * 128, :], (), [bkey], bkey[0] + str(bkey[1]))
                    tt("pool", W2g[:, mch, :], buf[:], G2[:], ALU.mult, [bkey, "G2"], [("W2g", mch)])
                if last and final_norm:
                    for half in range(2):
                        pb_, pk = psring.next()
                        mm(pb_[:], C("ones", 1), fgrow[0:1, half * 512:(half + 1) * 512], True, True,
                           ["cst", "fgrow"], [pk])
                        cp("dve", G2[:, half * 512:(half + 1) * 512], pb_[:], [pk], ["G2"])

                def front(t):
                    front_end(src, t * TB, 4, xin, xnr, hT, "hT", sc[:, 8:16], m[:, 24:32],
                              [("scol", l % 2, 1), ("modc", l % 2)])

                if dbg != "Bw":
                    front(0)
                for t in range(NTB if dbg != "Bw" else 0):
                    if dbg in ("B1", "F1", "F2"):
                        break
                    for mch in range(MC):
                        pa, pak = psring.next()
                        pb_, pbk = psring.next()
                        msl = slice(mch * 128, (mch + 1) * 128)
                        for c in range(KC):
                            mm(pa[:], W1[:, c, msl], hT[:, c, :], c == 0, c == KC - 1,
                               [("W1", mch // 2), ("hT", c)], [pak])
                        for c in range(KC):
                            mm(pb_[:], W3[:, c, msl], hT[:, c, :], c == 0, c == KC - 1,
                               [("W3", mch // 2), ("hT", c)], [pbk])
                        sa, sak = sar.next()
                        act(sa[:], pa[:], AF.Silu, [pak], [sak])
                        tt("dve", gT[:, mch, :], sa[:], pb_[:], ALU.mult, [sak, pbk], [("gT", mch)])
                    if dbg == "B2":
                        break
                    if l + 1 < DEPTH:
                        if t == 0:
                            load_pp(l + 1)
                        per = (8 + NTB - 1) // NTB
                        for kc in range(t * per, min(8, (t + 1) * per)):
                            mod_chunk(l + 1, kc, stg)
                    if t + 1 < NTB and dbg != "B3":
                        front(t + 1)
                    for j in range(4):
                        r0 = t * TB + j * 128
                        xr, xrk = xin.next()
                        dma("sp", xr[:], src[r0:r0 + 128, :], [("xrows", r0 // 128)], [xrk], xrk[0] + str(xrk[1]))
                        for n in range(2):
                            pf, pfk = psring.next()
                            for mch in range(MC):
                                mm(pf[:], gT[:, mch, j * 128:(j + 1) * 128], W2g[:, mch, n * 512:(n + 1) * 512],
                                   mch == 0, mch == MC - 1, [("gT", mch), ("W2g", mch)], [pfk])
                            tt("dve", xr[:, n * 512:(n + 1) * 512], pf[:], xr[:, n * 512:(n + 1) * 512], ALU.add,
                               [pfk, xrk], [xrk])
                        if last and final_norm:
                            jk, jkk = xnr.next()
                            T.op("act", (lambda b, xo: (lambda e: e.activation(
                                out=xo[:], in_=b[:], func=AF.Square, accum_out=ss_t[:, 8:9])))(xr, jk),
                                [xrk], [jkk, ("ss", 8)])
                            ts("dve", rstd_t[:, 8:9], ss_t[:, 8:9], 1.0 / D, EPS, ALU.mult, ALU.add,
                               [("ss", 8)], [("rstd", 8)])
                            act(rstd_t[:, 8:9], rstd_t[:, 8:9], AF.Sqrt, [("rstd", 8)], [("rstd", 8)])
                            recip(rstd_t[:, 8:9], rstd_t[:, 8:9], [("rstd", 8)], [("rstd", 8)])
                            stt(xr[:], xr[:], rstd_t[:, 8:9], G2[:], ALU.mult, ALU.mult,
                                [xrk, ("rstd", 8), "G2"], [xrk])
                            dma("sp", out[r0:r0 + 128, :], xr[:], [xrk], [("orows", r0 // 128)], ("ost", 4))
                        else:
                            dst = out if last else xs
                            dma("sp", dst[r0:r0 + 128, :], xr[:], [xrk], [("xrows", r0 // 128)], ("xst", 4))
                    if dbg == "B3":
                        break
                if l + 1 < DEPTH:
                    mod_finish(l + 1)
                T.barrier()

        NSA = TA // 128

        def phase_A(l, src, dstA):
            m = modc[l % 2]
            sc = scol[l % 2]
            ppk = ("pp", l % 2)
            with ExitStack() as ph:
                Win = sb("Win", [128, KC, NIN], BF16, ph)
                Wout = sb("Wout", [128, KC, D], BF16, ph)
                xin_t = [sb(f"xinA{i}", [128, 1024], F32, ph) for i in range(3)]
                xin = Ring([(xin_t[i], ("xin", i)) for i in range(3)])
                xn_t = [sb(f"xnA{i}", [128, 1024], BF16, ph) for i in range(NSA)]
                xnr = Ring([(xn_t[i], ("xn", i)) for i in range(NSA)])
                hT = sb("hTA", [128, KC, TA], BF16, ph)
                mixT = sb("mixTA", [128, KC, TA], BF16, ph)
                G1 = sb("G1", [128, D], F32, ph)
                ringA = Ring([(psF[i], ("psf", i)) for i in range(3)] + [(psM, "psM")])
                po = [(psF[3], ("psf", 3)), (psF[4], ("psf", 4))]
                f_t = [sb(f"fsc{i}", [128, TA], F32, ph) for i in range(10)]
                fr = Ring([(f_t[i], ("fsc", i)) for i in range(10)])
                b_t = [sb(f"bsc{i}", [128, 512], BF16, ph) for i in range(4)]
                br = Ring([(b_t[i], ("bsc", i)) for i in range(4)])

                gate_rows(l, 1, G1, "G1", xin_t[0], ("xin", 0))
                wv = w_in[l].rearrange("(c p) n -> p c n", p=128)
                segs = [(i * 512, min((i + 1) * 512, NIN)) for i in range(6)]
                for si, (c0, c1) in enumerate(segs):
                    dma("pool", Win[:, :, c0:c1], wv[:, :, c0:c1], (), [("Win", si)], ("Win", 6))

                def wk(c0, c1):
                    return [("Win", si) for si, (a0, a1) in enumerate(segs) if a0 < c1 and c0 < a1]

                for c in range(KC):
                    buf, bkey = xin.next()
                    dma("sp", buf[:], w_out[l, c * 128:(c + 1) * 128, :], (), [bkey], bkey[0] + str(bkey[1]))
                    tt("pool", Wout[:, c, :], buf[:], G1[:], ALU.mult, [bkey, "G1"], [("Wout", c)])

                def proj(c0, M):
                    ps, pk = ringA.next()
                    for c in range(KC):
                        mm(ps[0:M, 0:TA], Win[:, c, c0:c0 + M], hT[:, c, :], c == 0, c == KC - 1,
                           wk(c0, c0 + M) + [("hT", c)], [pk])
                    return ps, pk

                for c in range(KC):
                    mset("pool", mixT[:, c, :], 0.0, [("mixT", c)])

                if "pool" in mixers:
                    pbuf = [sb(f"pbuf{i}", [128, 2, 16 + TA], F32, ph) for i in range(2)]
                    pw_t = sb("pwin", [128, 4, 16 + TA], F32, ph)
                    pdel = sb("pdel", [128, 2, TA], BF16, ph)
                    pbd = sb("pbd", [128, 2, 128], BF16, ph)
                    cp("dve", pbd[:], PP(l, "poolbd").rearrange("p (a b) -> p a b", a=2), [ppk], ["pbd"])
                if "conv" in mixers:
                    hbuf = [sb(f"hbuf{i}", [128, 2, 30 + TA], BF16, ph) for i in range(2)]
                    Dg = sb("Dg", [128, 2, 31, 128], BF16, ph)
                    c32 = sb("c32", [128, 2, TA], F32, ph)
                    csq = sb("csq", [128, 2, TA], F32, ph)
                    cwo = PPO["cw"][0]
                    for ch in range(2):
                        for j in range(31):
                            ts("pool", Dg[:, ch, j, :], C("ident"),
                               pp[l % 2][:, cwo + ch * 31 + j: cwo + ch * 31 + j + 1], None, ALU.mult, None,
                               ["cst", ppk], [("Dg", ch)])
                if "ret" in mixers:
                    cs_t = sb("cs_t", [128, 2, TA], F32, ph)
                    qr = sb("qr", [128, 2, TA], BF16, ph)
                    kr = sb("kr", [128, 2, TA], BF16, ph)
                    qxi = sb("qxi", [128, 2, TA], BF16, ph)
                    gate = sb("rgate", [128, 2, TA], F32, ph)
                    rvp = sb("rvp", [128, NSA, 4, 128], BF16, ph)
                    rvu = sb("rvu", [128, NSA, 256], BF16, ph)
                    o32 = sb("o32", [128, 2, TA], F32, ph)
                    kz = sb("kz", [128, 2, 128], BF16, ph)
                    st32 = sb("st32", [128, 2, 64], F32, ph)
                    stpad = sb("stpad", [128, 2, 2, 128], BF16, ph)
                    mset("pool", rvp[:], 0.0, ["rvp"])
                    mset("pool", stpad[:], 0.0, ["stpad"])
                if "fox" in mixers:
                    qTa = sb("qTa", [128, 4, TA], BF16, ph)
                    kst = sb("kst", [128, 4, TA], BF16, ph)
                    vst = sb("vst", [128, NSA, 4, 128], BF16, ph)
                    ncum = [sb(f"ncum{i}", [4, TA], F32, ph) for i in range(2)]
                    fone = sb("fone", [4, TA], F32, ph)
                    c8 = sb("c8", [4, TA], BF16, ph)
                    nfb = sb("nfb", [4, 1], F32, ph)
                    bk = sb("bk", [128, S // 128, 4], F32, ph)
                    kr_t = [sb(f"kring{i}", [128, 4, TA], BF16, ph) for i in range(3)]
                    kring = Ring([(kr_t[i], ("kring", i)) for i in range(3)])
                    vr_t = [sb(f"vring{i}", [128, NSA, 4, 128], BF16, ph) for i in range(3)]
                    vring = Ring([(vr_t[i], ("vring", i)) for i in range(3)])
                    mset("pool", kst[:], 1.0, ["kst"])
                    mset("pool", vst[:], 1.0, [("vst", j) for j in range(NSA)])
                    mset("pool", qTa[:], 0.0, [("qTa", h) for h in range(4)] + ["qTa64"])
                    mset("dve", fone[:], 1.0, ["fone"])
                    ts("dve", nfb[:], PP(l, "fb", 4), -1.0, None, ALU.mult, None, [ppk], ["nfb"])

                def front(t):
                    front_end(src, t * TA, NSA, xin, xnr, hT, "hT", sc[:, 0:8], m[:, 0:8],
                              [("scol", l % 2, 0), ("modc", l % 2)])

                front(0)
                for t in range(NTA):
                    r0 = t * TA
                    cur, prv = t % 2, (t + 1) % 2
                    if "fox" in mixers or "ret" in mixers:
                        for j in range(NSA):
                            psv, pvk = ringA.next()
                            for gi, c0 in enumerate((FV, RV)):
                                for c in range(KC):
                                    mm(psv[:, gi * 256:(gi + 1) * 256], hT[:, c, j * 128:(j + 1) * 128],
                                       Win[:, c, c0:c0 + 256], c == 0, c == KC - 1,
                                       wk(c0, c0 + 256) + [("hT", c)], [pvk])
                            if "fox" in mixers:
                                vsv = vst[:, j].rearrange("p (a b) c -> p a b c", b=2)
                                pin = psv[:, 0:256].rearrange("p (a b c) -> p a b c", a=2, b=2)
                                cp("act", vsv[:, :, 0, 0:64], pin[:, :, 0, :], [pvk], [("vst", j)])
                                cp("act", vsv[:, :, 1, 64:128], pin[:, :, 1, :], [pvk], [("vst", j)])
                                dma("sp", vc_d[r0 + j * 128: r0 + (j + 1) * 128], vst[:, j], [("vst", j)],
                                    [("vc", (r0 // 128) + j)], ("vst", 2))
                            if "ret" in mixers:
                                rvv = rvp[:, j].rearrange("p (a b) c -> p a b c", b=2)
                                pin = psv[:, 256:512].rearrange("p (a b c) -> p a b c", a=2, b=2)
                                cp("dve", rvv[:, :, 0, 0:64], pin[:, :, 0, :], [pvk], ["rvp"])
                                cp("dve", rvv[:, :, 1, 64:128], pin[:, :, 1, :], [pvk], ["rvp"])
                                cp("dve", rvu[:, j, :], psv[:, 256:512], [pvk], [("rvu", j)])
                    if "pool" in mixers:
                        L = 16 + TA
                        for ch in range(2):
                            ps, pk = proj(PU + ch * 128, 128)
                            X = pbuf[cur][:, ch, :]
                            cp("act", X[:, 16:L], ps[:, 0:TA], [pk], [("pbuf", cur, ch)])
                            if t == 0:
                                mset("pool", X[:, 0:16], 0.0, [("pbufh", cur, ch)])
                            else:
                                cp("pool", X[:, 0:16], pbuf[prv][:, ch, TA:L], [("pbuf", prv, ch)],
                                   [("pbufh", cur, ch)])
                            xk = [("pbuf", cur, ch), ("pbufh", cur, ch)]
                            s2, s4, s8, s16 = (pw_t[:, i, :] for i in range(4))
                            tt("pool", s2[:, 1:L], X[:, 1:L], X[:, 0:L - 1], ALU.add, xk, [("pw", 0)])
                            tt("pool", s4[:, 3:L], s2[:, 3:L], s2[:, 1:L - 2], ALU.add, [("pw", 0)], [("pw", 1)])
                            if ch == 1:
                                tt("pool", s8[:, 7:L], s4[:, 7:L], s4[:, 3:L - 4], ALU.add, [("pw", 1)], [("pw", 2)])
                                tt("pool", s16[:, 15:L], s8[:, 15:L], s8[:, 7:L - 8], ALU.add, [("pw", 2)], [("pw", 3)])
                                srcs = ((s8, ("pw", 2)), (s16, ("pw", 3)))
                            else:
                                srcs = ((s2, ("pw", 0)), (s4, ("pw", 1)))
                            rw = C("poolrw")
                            for half, (sw, swk) in enumerate(srcs):
                                rs = slice(half * 64, (half + 1) * 64)
                                stt(pdel[rs, ch, :], sw[rs, 16:L], rw[rs, ch:ch + 1], X[rs, 16:L], ALU.mult,
                                    ALU.subtract, [swk, "cst"] + xk, [("pdel", ch)])
                                if t == 0:
                                    tmp, tk = fr.next()
                                    rc = C("rc16")
                                    tt("dve", tmp[rs, 0:16], sw[rs, 16:32], rc[rs, ch * 16:(ch + 1) * 16], ALU.mult,
                                       [swk, "cst"], [tk])
                                    tt("dve", pdel[rs, ch, 0:16], tmp[rs, 0:16], X[rs, 16:32], ALU.subtract,
                                       [tk] + xk, [("pdel", ch)])
                            ps2, pk2 = ringA.next()
                            mm(ps2[:, 0:TA], pbd[:, ch, :], pdel[:, ch, :], True, True, ["pbd", ("pdel", ch)], [pk2])
                            act(mixT[:, 2 + ch, :], ps2[:, 0:TA], AF.Copy, [pk2, ppk], [("mixT", 2 + ch)],
                                scale=PP(l, "pscale")[:, ch:ch + 1])
                    if "conv" in mixers:
                        Lc = 30 + TA
                        for ch in range(2):
                            psa, pak = proj(CA + ch * 128, 128)
                            psg, pgk = proj(CG + ch * 128, 128)
                            sg, sgk = fr.next()
                            act(sg[:], psg[:, 0:TA], AF.Sigmoid, [pgk], [sgk])
                            H = hbuf[cur][:, ch, :]
                            tt("dve", H[:, 30:Lc], psa[:, 0:TA], sg[:], ALU.mult, [pak, sgk], [("hbuf", cur, ch)])
                            if t == 0:
                                mset("pool", H[:, 0:30], 0.0, [("hbufh", cur, ch)])
                            else:
                                cp("pool", H[:, 0:30], hbuf[prv][:, ch, TA:Lc], [("hbuf", prv, ch)],
                                   [("hbufh", cur, ch)])
                            psc, pck = ringA.next()
                            for j in range(31):
                                mm(psc[:, 0:TA], Dg[:, ch, j, :], H[:, j:j + TA], j == 0, j == 30,
                                   [("Dg", ch), ("hbuf", cur, ch), ("hbufh", cur, ch)], [pck])
                            act(c32[:, ch, :], psc[:, 0:TA], AF.Identity, [pck, ppk], [("c32", ch)],
                                bias=PP(l, "cb")[:, ch:ch + 1])
                            act(csq[:, ch, :], psc[:, 0:TA], AF.Square, [pck, ppk], [("csq", ch)],
                                bias=PP(l, "cb")[:, ch:ch + 1])
                        pm, pmk = ringA.next()
                        mm(pm[:, 0:TA], C("o256"), c32[:, 0, :], True, False, ["cst", ("c32", 0)], [pmk])
                        mm(pm[:, 0:TA], C("o256"), c32[:, 1, :], False, True, ["cst", ("c32", 1)], [pmk])
                        pq, pqk = ringA.next()
                        mm(pq[:, 0:TA], C("o256"), csq[:, 0, :], True, False, ["cst", ("csq", 0)], [pqk])
                        mm(pq[:, 0:TA], C("o256"), csq[:, 1, :], False, True, ["cst", ("csq", 1)], [pqk])
                        msq, msk = fr.next()
                        act(msq[:], pm[:, 0:TA], AF.Square, [pmk], [msk])
                        var, vk = fr.next()
                        tt("dve", var[:], pq[:, 0:TA], msq[:], ALU.subtract, [pqk, msk], [vk])
                        ts("dve", var[:], var[:], 0.0, None, ALU.max, None, [vk], [vk]); act(var[:], var[:], AF.Sqrt, [vk], [vk], bias=EPS)
                        recip(var[:], var[:], [vk], [vk])
                        for ch in range(2):
                            dd, dk = fr.next()
                            tt("dve", dd[:], c32[:, ch, :], pm[:, 0:TA], ALU.subtract, [("c32", ch), pmk], [dk])
                            tt("dve", dd[:], dd[:], var[:], ALU.mult, [dk, vk], [dk])
                            act(mixT[:, 6 + ch, :], dd[:], AF.Silu, [dk, ppk], [("mixT", 6 + ch)],
                                scale=PP(l, "lng")[:, ch:ch + 1], bias=PP(l, "lnb")[:, ch:ch + 1])
                    if "ret" in mixers:
                        dma("sp", cs_t[:], cs_d[:, :, r0:r0 + TA].rearrange("a p s -> p a s"), ["cs_d"], ["cs_t"], "cs_t")
                        for pr in range(2):
                            for which, (c0, dst, dkey) in enumerate(((RQ, qr, "qr"), (RK, kr, "kr"))):
                                ps, pk = proj(c0 + pr * 128, 128)
                                r32, rk32 = fr.next()
                                cp("act", r32[:], ps[:, 0:TA], [pk], [rk32])
                                prt, prk = ringA.next()
                                mm(prt[:, 0:TA], C("rm"), r32[:], True, True, ["cst", rk32], [prk])
                                t1, t1k = fr.next()
                                tt("pool", t1[:], r32[:], cs_t[:, 0, :], ALU.mult, [rk32, "cs_t"], [t1k])
                                t2, t2k = fr.next()
                                tt("dve", t2[:], prt[:, 0:TA], cs_t[:, 1, :], ALU.mult, [prk, "cs_t"], [t2k])
                                tt("dve", t1[:], t1[:], t2[:], ALU.add, [t1k, t2k], [t1k])
                                cp("pool", dst[:, pr, :], t1[:], [t1k], [(dkey, pr)])
                                if which == 0:
                                    xiv = C("xi")[:, pr * 128:(pr + 1) * 128]
                                    for n in range(NSA):
                                        tt("dve", qxi[:, pr, n * 128:(n + 1) * 128], t1[:, n * 128:(n + 1) * 128], xiv,
                                           ALU.mult, [t1k, "cst"], [("qxi", pr)])
                            psg, pgk = proj(RG + pr * 128, 128)
                            act(gate[:, pr, :], psg[:, 0:TA], AF.Silu, [pgk], [("gate", pr)])
                        for n in range(NSA):
                            g = t * NSA + n
                            ns = slice(n * 128, (n + 1) * 128)
                            pssE, pskE = ringA.next()
                            pssO, pskO = ringA.next()
                            Pr, Prk = br.next()
                            Prv = Pr[:].rearrange("p (a b c) -> p a b c", a=2, b=2)
                            dmv = C("dmask").rearrange("p (a b c) -> p a b c", a=2, b=2)
                            for half, (pss, psk) in enumerate(((pssE, pskE), (pssO, pskO))):
                                rs = slice(half * 64, (half + 1) * 64)
                                for pr in range(2):
                                    mm(pss[:, pr * 128:(pr + 1) * 128], kr[rs, pr, ns], qr[rs, pr, ns], True, True,
                                       [("kr", pr), ("qr", pr)], [psk])
                            for half, (pss, psk) in enumerate(((pssE, pskE), (pssO, pskO))):
                                tt("dve", Prv[:, :, half, :], pss[:, 0:256].rearrange("p (a c) -> p a c", a=2),
                                   dmv[:, :, half, :], ALU.mult, [psk, "cst"], [Prk])
                            pso, pok = ringA.next()
                            for pr in range(2):
                                for half in range(2):
                                    h = 2 * pr + half
                                    rs = slice(half * 64, (half + 1) * 64)
                                    lastmm = (half == 1) and (g == 0)
                                    mm(pso[:, pr * 128:(pr + 1) * 128], rvp[:, n, h, :], Pr[:, h * 128:(h + 1) * 128],
                                       half == 0, lastmm, ["rvp", Prk], [pok])
                                    if g > 0:
                                        mm(pso[:, pr * 128:(pr + 1) * 128], stpad[rs, pr, half, :], qxi[rs, pr, ns],
                                           False, half == 1, ["stpad", ("qxi", pr)], [pok])
                                cp("act", o32[:, pr, ns], pso[:, pr * 128:(pr + 1) * 128], [pok], [("o32", pr)])
                            pkv, pkk = ringA.next()
                            for pr in range(2):
                                ptb, ptk = psbring.next()
                                tr(ptb[:, 0:128], kr[:, pr, ns], identb[:], [("kr", pr), "identb"], [ptk])
                                tt("dve", kz[:, pr, :], ptb[:, 0:128], C("zeta")[:, pr * 128:(pr + 1) * 128], ALU.mult,
                                   [ptk, "cst"], [("kz", pr)])
                                mm(pkv[:, pr * 128:(pr + 1) * 128], kz[:, pr, :], rvu[:, n, pr * 128:(pr + 1) * 128],
                                   True, True, [("kz", pr), ("rvu", n)], [pkk])
                            for pr in range(2):
                                for half in range(2):
                                    rs = slice(half * 64, (half + 1) * 64)
                                    kvs = pkv[rs, pr * 128 + half * 64: pr * 128 + (half + 1) * 64]
                                    if g == 0:
                                        cp("dve", st32[rs, pr, :], kvs, [pkk], ["st32"])
                                    else:
                                        stt(st32[rs, pr, :], st32[rs, pr, :], C("decay")[rs, pr:pr + 1], kvs,
                                            ALU.mult, ALU.add, ["st32", "cst", pkk], ["st32"])
                                    cp("pool", stpad[rs, pr, half, half * 64:(half + 1) * 64], st32[rs, pr, :],
                                       ["st32"], ["stpad"])
                        for pr in range(2):
                            osq, osk = fr.next()
                            act(osq[:], o32[:, pr, :], AF.Square, [("o32", pr)], [osk])
                            pm, pmk = ringA.next()
                            mm(pm[:, 0:TA], C("bd64"), o32[:, pr, :], True, True, ["cst", ("o32", pr)], [pmk])
                            pq, pqk = ringA.next()
                            mm(pq[:, 0:TA], C("bd64"), osq[:], True, True, ["cst", osk], [pqk])
                            msq, msk = fr.next()
                            act(msq[:], pm[:, 0:TA], AF.Square, [pmk], [msk])
                            var, vk = fr.next()
                            tt("dve", var[:], pq[:, 0:TA], msq[:], ALU.subtract, [pqk, msk], [vk])
                            ts("dve", var[:], var[:], 0.0, None, ALU.max, None, [vk], [vk]); act(var[:], var[:], AF.Sqrt, [vk], [vk], bias=EPS)
                            recip(var[:], var[:], [vk], [vk])
                            dd, dk = fr.next()
                            tt("dve", dd[:], o32[:, pr, :], pm[:, 0:TA], ALU.subtract, [("o32", pr), pmk], [dk])
                            tt("dve", dd[:], dd[:], var[:], ALU.mult, [dk, vk], [dk])
                            stt(mixT[:, 4 + pr, :], dd[:], PP(l, "gng")[:, pr:pr + 1], gate[:, pr, :], ALU.mult, ALU.mult,
                                [dk, ppk, ("gate", pr)], [("mixT", 4 + pr)])
                    if "fox" in mixers:
                        psf_, pfk_ = ringA.next()
                        for c in range(KC):
                            mm(psf_[0:4, 0:TA], Win[:, c, FFO:FFO + 4], hT[:, c, :], c == 0, c == KC - 1,
                               wk(FFO, FFO + 4) + [("hT", c)], [pfk_])
                        ee, eek = fr.next()
                        act(ee[0:4, :], psf_[0:4, 0:TA], AF.Exp, [pfk_, "nfb"], [eek], scale=-1.0, bias=nfb[:, 0:1])
                        act(ee[0:4, :], ee[0:4, :], AF.Ln, [eek], [eek], bias=1.0)
                        nc_cur, nc_prv = ncum[cur], ncum[prv]
                        init = 0.0 if t == 0 else nc_prv[:, TA - 1:TA]
                        T.op("dve", (lambda o_, d1, ini: (lambda e: e.tensor_tensor_scan(
                            out=o_[:], data0=fone[:], data1=d1[0:4, :], initial=ini, op0=ALU.mult, op1=ALU.add)))(
                            nc_cur, ee, init), [eek, "fone", ("ncum", prv)], [("ncum", cur)])
                        ts("dve", c8[:], nc_cur[:], -8.0, None, ALU.mult, None, [("ncum", cur)], ["c8"])
                        for h in range(4):
                            dma("sp", qTa[64:65, h, :], c8[h:h + 1, :], ["c8"], ["qTa64"], ("c8", 4))
                        for j in range(NSA):
                            ptt, ptk = ringA.next()
                            tr(ptt[:, 0:4], nc_cur[0:4, j * 128:(j + 1) * 128], C("ident")[0:4, 0:4],
                               [("ncum", cur), "cst"], [ptk])
                            cp("dve", bk[:, t * NSA + j, :], ptt[:, 0:4], [ptk], [("bk", t * NSA + j)])
                        for h in range(4):
                            ps, pk = proj(FQ + h * 64, 64)
                            cp("act", qTa[0:64, h, :], ps[0:64, 0:TA], [pk], [("qTa", h)])
                            ps, pk = proj(FK + h * 64, 64)
                            cp("dve", kst[0:64, h, :], ps[0:64, 0:TA], [pk], ["kst"])
                        dma("sp", kc_d[:, :, r0:r0 + TA], kst[0:65, :, :], ["kst"], [("kc", t)], ("kst", 2))
                        pend = []
                        first_bank = [True, True]

                        def flush_pv(item):
                            h_, q0_, Pt_, Ptk_, vb_, vbk_, b_, lastflag = item
                            bank, bkey_ = po[h_ // 2]
                            c0_ = (h_ % 2) * TA
                            st_ = first_bank[h_ // 2]
                            first_bank[h_ // 2] = False
                            mm(bank[:, c0_ + q0_: c0_ + TA], vb_[:, b_, h_, :], Pt_[:, q0_:TA], st_, lastflag,
                               [vbk_, Ptk_], [bkey_], skip_group_check=True)

                        for i in range(t + 1):
                            kb_, kbk = kring.next()
                            dma("sp", kb_[0:65, :, :], kc_d[:, :, i * TA:(i + 1) * TA], [("kc", i)], [kbk],
                                kbk[0] + str(kbk[1]))
                            vb_, vbk = vring.next()
                            dma("sp", vb_[:], vc_d[i * TA:(i + 1) * TA].rearrange("(b p) h c -> p b h c", p=128),
                                [("vc", i * NSA + jj) for jj in range(NSA)], [vbk], vbk[0] + str(vbk[1]))
                            diag = (i == t)
                            for b in range(NSA):
                                kbi = i * NSA + b
                                q0 = b * 128 if diag else 0
                                for h in range(4):
                                    pss, psk = ringA.next()
                                    mm(pss[:, q0:TA], kb_[0:65, h, b * 128:(b + 1) * 128], qTa[0:65, h, q0:TA],
                                       True, not diag, [kbk, ("qTa", h), "qTa64"], [psk])
                                    if diag:
                                        mm(pss[:, b * 128:(b + 1) * 128], identb[:], maskb[:], False, True,
                                           ["identb", "maskb"], [psk])
                                    Pt, Ptk = br.next()
                                    act(Pt[:, q0:TA], pss[:, q0:TA], AF.Exp, [psk, ("bk", kbi)], [Ptk],
                                        scale=0.125, bias=bk[:, kbi, h:h + 1])
                                    lastflag = diag and (b == NSA - 1) and (h % 2 == 1)
                                    pend.append((h, q0, Pt, Ptk, vb_, vbk, b, lastflag))
                                    if len(pend) > 1:
                                        flush_pv(pend.pop(0))
                        while pend:
                            flush_pv(pend.pop(0))
                        for h in range(4):
                            bank, bkey_ = po[h // 2]
                            c0_ = (h % 2) * TA
                            nrows = slice((h % 2) * 64, (h % 2) * 64 + 64)
                            drows = slice(64 - (h % 2) * 64, 128 - (h % 2) * 64)
                            rc_, rck = fr.next()
                            recip(rc_[drows, :], bank[drows, c0_:c0_ + TA], [bkey_], [rck])
                            px, pxk = ringA.next()
                            mm(px[:, 0:TA], C("sh")[drows, :], rc_[drows, :], True, True, ["cst", rck], [pxk])
                            nm, nmk = fr.next()
                            cp("act", nm[nrows, :], bank[nrows, c0_:c0_ + TA], [bkey_], [nmk])
                            tt("dve", mixT[nrows, h // 2, :], nm[nrows, :], px[nrows, 0:TA], ALU.mult,
                               [nmk, pxk], [("mixT", h // 2)])
                    if t + 1 < NTA:
                        front(t + 1)
                    for j in range(NSA):
                        rr = r0 + j * 128
                        xr, xrk = xin.next()
                        dma("sp", xr[:], src[rr:rr + 128, :], [("xrows", rr // 128)], [xrk], xrk[0] + str(xrk[1]))
                        for n in range(2):
                            pf, pfk = ringA.next()
                            for c in range(KC):
                                mm(pf[:], mixT[:, c, j * 128:(j + 1) * 128], Wout[:, c, n * 512:(n + 1) * 512],
                                   c == 0, c == KC - 1, [("mixT", c), ("Wout", c)], [pfk])
                            tt("dve", xr[:, n * 512:(n + 1) * 512], pf[:], xr[:, n * 512:(n + 1) * 512], ALU.add,
                               [pfk, xrk], [xrk])
                        dma("sp", dstA[rr:rr + 128, :], xr[:], [xrk], [("xrows", rr // 128)], ("xst", 4))
                T.barrier()

        with ExitStack() as pro:
            stg_t = [sb(f"stgP{i}", [128, 1024], F32, pro) for i in range(2)]
            stgp = Ring([(stg_t[i], ("stg", i)) for i in range(2)])
            load_pp(0)
            for kc in range(8):
                mod_chunk(0, kc, stgp)
            mod_finish(0)
            if "A" in phases and "ret" in mixers:
                posi = sb("posi", [128, S], I32, pro)
                ang = sb("ang", [128, 2, S], F32, pro)
                nfl = sb("nfl", [128, 2, S], F32, pro)
                nint = sb("nint", [128, 2, S], I32, pro)
                dma("sp", posi[:], pos_in[0].partition_broadcast(128), (), ["posi"], "posi")
                cp("dve", ang[:, 1, :], posi[:], ["posi"], ["ang1"])
                ts("dve", ang[:, 1, :], ang[:, 1, :], C("invf")[:, 0:1], None, ALU.mult, None, ["ang1", "cst"], ["ang1"])
                ts("dve", ang[:, 0, :], ang[:, 1, :], float(np.pi / 2), None, ALU.add, None, ["ang1"], ["ang0"])
                ak = ["ang0", "ang1"]
                ts("dve", nfl[:], ang[:], float(1.0 / (2 * np.pi)), None, ALU.mult, None, ak, ["nfl"])
                cp("dve", nint[:], nfl[:], ["nfl"], ["nint"])
                cp("dve", nfl[:], nint[:], ["nint"], ["nfl"])
                C1 = 6.28125
                C2 = float(2 * np.pi - 6.28125)
                stt(ang[:], nfl[:], -C1, ang[:], ALU.mult, ALU.add, ["nfl"] + ak, ak)
                stt(ang[:], nfl[:], -C2, ang[:], ALU.mult, ALU.add, ["nfl"] + ak, ak)
                ts("dve", nfl[:], ang[:], float(np.pi), float(2 * np.pi), ALU.is_gt, ALU.mult, ak, ["nfl"])
                tt("dve", ang[:], ang[:], nfl[:], ALU.subtract, ak + ["nfl"], ak)
                ts("dve", nfl[:], ang[:], float(-np.pi), float(2 * np.pi), ALU.is_lt, ALU.mult, ak, ["nfl"])
                tt("dve", ang[:], ang[:], nfl[:], ALU.add, ak + ["nfl"], ak)
                ts("dve", ang[:], ang[:], float(np.pi), float(-np.pi), ALU.min, ALU.max, ak, ak)
                act(ang[:], ang[:], AF.Sin, ak, ak)
                dma("sp", cs_d.rearrange("a p s -> p a s"), ang[:], ak, ["cs_d"], "cs_d")
            T.barrier()

        cur = x_in
        for l in range(DEPTH):
            if "A" in phases:
                phase_A(l, cur, out if (l == DEPTH - 1 and "B" not in phases) else xs)
                cur = xs
            if "B" in phases:
                phase_B(l, cur, l == DEPTH - 1)
                cur = xs
        T.op("sp", lambda e: e.nop(), [("orows", i) for i in range(S // 128)] + [("xrows", i) for i in range(S // 128)], ())
        T.barrier()
    return


def build(S, DEPTH, **kw):
    nc0 = bass.Bass("TRN2", target_bir_lowering=False)
    t0 = Trk()
    build_program(nc0, t0, S, DEPTH, **kw)
    plan = t0.finish()
    nc = bass.Bass("TRN2", target_bir_lowering=False)
    es = ExitStack()
    sems = {}
    for k in t0.semkeys:
        sems[k] = es.enter_context(nc.semaphore(k.replace(":", "_")))
    t1 = Trk(plan, nc, lambda k: sems[k])
    build_program(nc, t1, S, DEPTH, **kw)
    assert t1.n == len(plan), (t1.n, len(plan))
    return nc, t0


def make_in_maps(inp, S, DEPTH):
    cst = make_consts()
    pp = make_pp(inp, DEPTH)
    maps = []
    f = lambda a: np.ascontiguousarray(np.asarray(a, np.float32))
    shared = {
        "ada_w": f(inp["ada_w"][:DEPTH]), "w_in": f(inp["w_in"][:DEPTH]), "w_out": f(inp["w_out"][:DEPTH]),
        "ffn_w1": f(inp["ffn_w1"][:DEPTH]), "ffn_w3": f(inp["ffn_w3"][:DEPTH]), "ffn_w2": f(inp["ffn_w2"][:DEPTH]),
        "pp": pp, "cst": cst, "fg": f(inp["final_g"]).reshape(1, D),
    }
    x = np.asarray(inp["x"], np.float32)
    c = np.asarray(inp["c"], np.float32)
    pos = np.asarray(inp["positions"]).astype(np.int32)
    for b in range(x.shape[0]):
        m = dict(shared)
        m["x"] = np.ascontiguousarray(x[b, :S])
        m["ccol"] = np.ascontiguousarray(c[b].reshape(8, 128).T)
        m["pos"] = np.ascontiguousarray(pos[b, :S].reshape(1, S))
        maps.append(m)
    return maps


def kernel(**inputs):
    S = inputs["x"].shape[1]
    DEPTH = inputs["ada_w"].shape[0]
    nc, _ = build(S, DEPTH)
    maps = make_in_maps(inputs, S, DEPTH)
    res = run_bass_kernel_spmd(nc, maps, core_ids=list(range(len(maps))))
    return np.stack([r["out"] for r in res.results], axis=0)
```
